# Optimizing a Trainium2 kernel written in Bass

```python
import jax, jax.numpy as jnp
from jax import lax
import numpy as np

D_MODEL = 2048
BATCH = 4
SEQ = 4096
DEPTH = 2

CTX_LEN = 256
GRID_W = 64
N_DIR = 2
MIX_WIDTH = D_MODEL
GLA_HEADS = 4
GLA_WIDTH = MIX_WIDTH // 2
GLA_DV = GLA_WIDTH // GLA_HEADS
GLA_DK = GLA_DV // 2
GLA_LR = 16
GLA_NORMALIZER = 16.0
MLSTM_HEADS = 4
MLSTM_WIDTH = MIX_WIDTH - GLA_WIDTH
MLSTM_DV = MLSTM_WIDTH // MLSTM_HEADS
MLSTM_DK = MLSTM_DV // 2
CHUNK = 64
D_FF = 128 * ((8 * D_MODEL // 3 + 127) // 128)
CONV_K = 3
EPS = 1e-6

PROJ_SIZES = (GLA_HEADS * GLA_DK, GLA_HEADS * GLA_DK, GLA_WIDTH, GLA_WIDTH, GLA_LR,
              MLSTM_HEADS * MLSTM_DK, MLSTM_HEADS * MLSTM_DK, MLSTM_WIDTH, MLSTM_WIDTH,
              N_DIR * 2 * MLSTM_HEADS)
PROJ_WIDTH = sum(PROJ_SIZES)
PROJ_SPLITS = tuple(int(s) for s in np.cumsum(PROJ_SIZES)[:-1])

kernel_name = 'hymba_gla_mlstm_convffn_prefix_block'


def rmsnorm(x, g):
    xf = x.astype(jnp.float32)
    y = xf * lax.rsqrt(jnp.mean(jnp.square(xf), axis=-1, keepdims=True) + EPS)
    return (y * g.astype(jnp.float32)).astype(x.dtype)


def _rev(t, d):
    return jnp.flip(t, axis=1) if d == 1 else t


def _chunk(t):
    b, t_len = t.shape[:2]
    t = t.reshape((b, t_len // CHUNK, CHUNK) + t.shape[2:])
    return jnp.swapaxes(jnp.moveaxis(t, 1, 0), 2, 3)


def _unchunk(t):
    n, b, h, c, d = t.shape
    return jnp.moveaxis(jnp.swapaxes(t, 2, 3), 0, 1).reshape(b, n * c, h, d)


def gla_scan(k, v, log_a, s0, q=None):
    tril = jnp.tril(jnp.ones((CHUNK, CHUNK), dtype=bool))
    xs = (_chunk(k), _chunk(v), _chunk(log_a)) + ((_chunk(q),) if q is not None else ())

    def step(s, inp):
        kc, vc, ac = inp[:3]
        bcum = jnp.cumsum(ac, axis=2)
        b_end = bcum[:, :, -1, :]
        k_to_end = kc * jnp.exp(b_end[:, :, None, :] - bcum)
        s_new = jnp.exp(b_end)[..., None] * s + jnp.einsum('bhik,bhiv->bhkv', k_to_end, vc)
        if q is None:
            return s_new, None
        qc = inp[3]
        rel = jnp.where(tril[:, :, None], bcum[:, :, :, None, :] - bcum[:, :, None, :, :], -jnp.inf)
        scores = jnp.einsum('bhjk,bhik,bhjik->bhji', qc, kc, jnp.exp(rel))
        o = (jnp.einsum('bhji,bhiv->bhjv', scores, vc)
             + jnp.einsum('bhjk,bhkv->bhjv', qc * jnp.exp(bcum), s))
        return s_new, o

    s_fin, o = lax.scan(step, s0, xs)
    return s_fin, (None if q is None else _unchunk(o))


def mlstm_scan(k, v, log_i, log_f, state0, q=None):
    tril = jnp.tril(jnp.ones((CHUNK, CHUNK), dtype=bool))
    xs = (_chunk(k), _chunk(v), _chunk(log_i), _chunk(log_f)) + ((_chunk(q),) if q is not None else ())

    def step(carry, inp):
        c, n, m = carry
        kc, vc, ic, fc = inp[:4]
        fcum = jnp.cumsum(fc, axis=-1)
        f_end = fcum[..., -1]
        w_log = f_end[..., None] - fcum + ic
        m_new = jnp.maximum(f_end + m, jnp.max(w_log, axis=-1))
        carry_scale = jnp.exp(f_end + m - m_new)
        w = jnp.exp(w_log - m_new[..., None])
        c_new = carry_scale[..., None, None] * c + jnp.einsum('bhi,bhik,bhiv->bhkv', w, kc, vc)
        n_new = carry_scale[..., None] * n + jnp.einsum('bhi,bhik->bhk', w, kc)
        if q is None:
            return (c_new, n_new, m_new), None
        qc = inp[4]
        d_log = jnp.where(tril, fcum[..., :, None] - fcum[..., None, :] + ic[..., None, :], -jnp.inf)
        inter_log = fcum + m[..., None]
        m_q = jnp.maximum(inter_log, jnp.max(d_log, axis=-1))
        s = jnp.einsum('bhjk,bhik->bhji', qc, kc) * jnp.exp(d_log - m_q[..., None])
        inter = jnp.exp(inter_log - m_q)
        num = (jnp.einsum('bhji,bhiv->bhjv', s, vc)
               + inter[..., None] * jnp.einsum('bhjk,bhkv->bhjv', qc, c))
        den = jnp.sum(s, axis=-1) + inter * jnp.einsum('bhjk,bhk->bhj', qc, n)
        h = num / jnp.maximum(jnp.abs(den), jnp.exp(-m_q))[..., None]
        return (c_new, n_new, m_new), h

    st_fin, h = lax.scan(step, state0, xs)
    return st_fin, (None if q is None else _unchunk(h))


def project_heads(h, w_in, gla_w_lr, gla_b_lr, mlstm_b_gate):
    b, t, _ = h.shape
    f32 = jnp.float32
    z = jnp.einsum('btd,dp->btp', h, w_in)
    gq, gk, gv, gg, glr, mq, mk, mv, mo, mgate = jnp.split(z, PROJ_SPLITS, axis=-1)
    gla_q = gq.reshape(b, t, GLA_HEADS, GLA_DK).astype(f32) * (GLA_DK ** -0.5)
    gla_k = gk.reshape(b, t, GLA_HEADS, GLA_DK).astype(f32)
    gla_v = gv.reshape(b, t, GLA_HEADS, GLA_DV).astype(f32)
    gla_gate = gg.reshape(b, t, GLA_HEADS, GLA_DV)
    dec = jnp.einsum('btr,zrk->zbtk', glr, gla_w_lr) + gla_b_lr[:, None, None, :]
    gla_loga = (jax.nn.log_sigmoid(dec.astype(f32)) / GLA_NORMALIZER).reshape(N_DIR, b, t, GLA_HEADS, GLA_DK)
    ml_q = mq.reshape(b, t, MLSTM_HEADS, MLSTM_DK).astype(f32)
    ml_k = mk.reshape(b, t, MLSTM_HEADS, MLSTM_DK).astype(f32) * (MLSTM_DK ** -0.5)
    ml_v = mv.reshape(b, t, MLSTM_HEADS, MLSTM_DV).astype(f32)
    ml_ogate = mo.reshape(b, t, MLSTM_HEADS, MLSTM_DV)
    pre = (mgate.reshape(b, t, N_DIR, 2, MLSTM_HEADS) + mlstm_b_gate).astype(f32)
    ml_logi = jnp.moveaxis(pre[:, :, :, 0], 2, 0)
    ml_logf = jnp.moveaxis(jax.nn.log_sigmoid(pre[:, :, :, 1]), 2, 0)
    scan_in = (gla_q, gla_k, gla_v, gla_loga, ml_q, ml_k, ml_v, ml_logi, ml_logf)
    return scan_in, (gla_gate, ml_ogate)


def recurrent_groups(lat, ctx, with_ctx_out):
    gq, gk, gv, ga, mq, mk, mv, mi, mf = lat
    cq, ck, cv, ca, cmq, cmk, cmv, cmi, cmf = ctx
    b = gq.shape[0]
    f32 = jnp.float32
    gla_lat, ml_lat, gla_ctx, ml_ctx = [], [], [], []
    for d in range(N_DIR):
        s0 = jnp.zeros((b, GLA_HEADS, GLA_DK, GLA_DV), f32)
        s_ctx, o_ctx = gla_scan(_rev(ck, d), _rev(cv, d), _rev(ca[d], d), s0,
                                _rev(cq, d) if with_ctx_out else None)
        _, o_lat = gla_scan(_rev(gk, d), _rev(gv, d), _rev(ga[d], d), s_ctx, _rev(gq, d))
        gla_lat.append(_rev(o_lat, d))
        st0 = (jnp.zeros((b, MLSTM_HEADS, MLSTM_DK, MLSTM_DV), f32),
               jnp.zeros((b, MLSTM_HEADS, MLSTM_DK), f32),
               jnp.zeros((b, MLSTM_HEADS), f32))
        st_ctx, h_ctx = mlstm_scan(_rev(cmk, d), _rev(cmv, d), _rev(cmi[d], d), _rev(cmf[d], d), st0,
                                   _rev(cmq, d) if with_ctx_out else None)
        _, h_lat = mlstm_scan(_rev(mk, d), _rev(mv, d), _rev(mi[d], d), _rev(mf[d], d), st_ctx, _rev(mq, d))
        ml_lat.append(_rev(h_lat, d))
        if with_ctx_out:
            gla_ctx.append(_rev(o_ctx, d))
            ml_ctx.append(_rev(h_ctx, d))
    lat_out = (gla_lat[0] + gla_lat[1], ml_lat[0] + ml_lat[1])
    ctx_out = (gla_ctx[0] + gla_ctx[1], ml_ctx[0] + ml_ctx[1]) if with_ctx_out else None
    return lat_out, ctx_out


def merge_groups(gla_o, ml_h, gla_gate, ml_ogate, gla_g_norm, mlstm_g_norm, w_out, dtype):
    b, t = gla_o.shape[:2]
    f32 = jnp.float32
    y_gla = rmsnorm(gla_o, gla_g_norm) * jax.nn.silu(gla_gate.astype(f32))
    y_ml = jax.nn.sigmoid(ml_ogate.astype(f32)) * rmsnorm(ml_h, mlstm_g_norm)
    y = jnp.concatenate([y_gla.reshape(b, t, GLA_WIDTH), y_ml.reshape(b, t, MLSTM_WIDTH)], axis=-1)
    return jnp.einsum('btm,md->btd', y.astype(dtype), w_out)


def conv_ffn(h, rows, cols, w_up, conv_w, conv_b, w_down):
    b, t, _ = h.shape
    u = jnp.einsum('btd,df->btf', h, w_up)
    u_gate, u_val = jnp.split(u, 2, axis=-1)
    g = lax.conv_general_dilated(u_gate.reshape(b, rows, cols, D_FF), conv_w[:, :, None, :],
                                 window_strides=(1, 1), padding='SAME',
                                 dimension_numbers=('NHWC', 'HWIO', 'NHWC'),
                                 feature_group_count=D_FF)
    g = g.reshape(b, t, D_FF) + conv_b
    return jnp.einsum('btf,fd->btd', jax.nn.silu(g) * u_val, w_down)


def trunk_layer(x, ctx, c_act, cc_act, w_mod, b_mod, g_norm1, g_norm2, w_in, gla_w_lr, gla_b_lr,
                mlstm_b_gate, gla_g_norm, mlstm_g_norm, w_out, w_up, conv_w, conv_b, w_down, update_ctx):
    b, t, _ = x.shape
    rows = t // GRID_W
    mod = (c_act @ w_mod + b_mod)[:, None, :]
    mod_c = (cc_act @ w_mod + b_mod)[None, None, :]
    sh1, sc1, gt1, sh2, sc2, gt2 = jnp.split(mod, 6, axis=-1)
    csh1, csc1, cgt1, csh2, csc2, cgt2 = jnp.split(mod_c, 6, axis=-1)

    h = rmsnorm(x, g_norm1) * (1 + sc1) + sh1
    hc = rmsnorm(ctx, g_norm1) * (1 + csc1) + csh1
    lat_in, lat_gates = project_heads(h, w_in, gla_w_lr, gla_b_lr, mlstm_b_gate)
    ctx_in, ctx_gates = project_heads(hc, w_in, gla_w_lr, gla_b_lr, mlstm_b_gate)
    (lat_gla, lat_ml), ctx_out = recurrent_groups(lat_in, ctx_in, update_ctx)
    x = x + gt1 * merge_groups(lat_gla, lat_ml, lat_gates[0], lat_gates[1],
                               gla_g_norm, mlstm_g_norm, w_out, x.dtype)

    h2 = rmsnorm(x, g_norm2) * (1 + sc2) + sh2
    x = x + gt2 * conv_ffn(h2, rows, GRID_W, w_up, conv_w, conv_b, w_down)

    if update_ctx:
        ctx = ctx + cgt1 * merge_groups(ctx_out[0], ctx_out[1], ctx_gates[0], ctx_gates[1],
                                        gla_g_norm, mlstm_g_norm, w_out, ctx.dtype)
        hc2 = rmsnorm(ctx, g_norm2) * (1 + csc2) + csh2
        ctx = ctx + cgt2 * conv_ffn(hc2, 1, ctx.shape[1], w_up, conv_w, conv_b, w_down)
    return x, ctx


def setup_inputs(seed: int = 0) -> dict:
    key = jax.random.key(seed)
    ks = jax.random.split(key, 20)
    f32 = jnp.float32
    L = DEPTH

    def nrm(k, shape, s):
        return jax.random.normal(k, shape, f32) * s

    gate_base = jnp.array([0.0, 3.0], f32)[None, None, :, None]
    return {
        'x': nrm(ks[0], (BATCH, SEQ, D_MODEL), 1.0),
        'c': nrm(ks[1], (BATCH, D_MODEL), 1.0),
        'ctx': nrm(ks[2], (BATCH, CTX_LEN, D_MODEL), 1.0),
        'c_ctx': nrm(ks[3], (D_MODEL,), 1.0),
        'w_mod': nrm(ks[4], (L, D_MODEL, 6 * D_MODEL), D_MODEL ** -0.5),
        'b_mod': nrm(ks[5], (L, 6 * D_MODEL), 0.01),
        'g_norm1': 1.0 + nrm(ks[6], (L, D_MODEL), 0.05),
        'g_norm2': 1.0 + nrm(ks[7], (L, D_MODEL), 0.05),
        'w_in': nrm(ks[8], (L, D_MODEL, PROJ_WIDTH), D_MODEL ** -0.5),
        'gla_w_lr': nrm(ks[9], (L, N_DIR, GLA_LR, GLA_HEADS * GLA_DK), GLA_LR ** -0.5),
        'gla_b_lr': nrm(ks[10], (L, N_DIR, GLA_HEADS * GLA_DK), 0.1),
        'mlstm_b_gate': gate_base + nrm(ks[11], (L, N_DIR, 2, MLSTM_HEADS), 0.1),
        'gla_g_norm': 1.0 + nrm(ks[12], (L, GLA_DV), 0.05),
        'mlstm_g_norm': 1.0 + nrm(ks[13], (L, MLSTM_DV), 0.05),
        'w_out': nrm(ks[14], (L, MIX_WIDTH, D_MODEL), MIX_WIDTH ** -0.5),
        'w_up': nrm(ks[15], (L, D_MODEL, 2 * D_FF), D_MODEL ** -0.5),
        'conv_w': nrm(ks[16], (L, CONV_K, CONV_K, D_FF), 1.0 / CONV_K),
        'conv_b': nrm(ks[17], (L, D_FF), 0.01),
        'w_down': nrm(ks[18], (L, D_FF, D_MODEL), D_FF ** -0.5),
        'g_final': 1.0 + nrm(ks[19], (D_MODEL,), 0.05),
    }


def reference(x, c, ctx, c_ctx, w_mod, b_mod, g_norm1, g_norm2, w_in, gla_w_lr, gla_b_lr, mlstm_b_gate,
              gla_g_norm, mlstm_g_norm, w_out, w_up, conv_w, conv_b, w_down, g_final):
    c_act = jax.nn.silu(c)
    cc_act = jax.nn.silu(c_ctx)
    for l in range(DEPTH):
        x, ctx = trunk_layer(x, ctx, c_act, cc_act, w_mod[l], b_mod[l], g_norm1[l], g_norm2[l], w_in[l],
                             gla_w_lr[l], gla_b_lr[l], mlstm_b_gate[l], gla_g_norm[l], mlstm_g_norm[l],
                             w_out[l], w_up[l], conv_w[l], conv_b[l], w_down[l],
                             update_ctx=(l < DEPTH - 1))
    return rmsnorm(x, g_final)
```

```python
import numpy as np
import concourse.bass as bass
import concourse.mybir as mybir
from concourse.bass_utils import run_bass_kernel_spmd

F32 = mybir.dt.float32
BF16 = mybir.dt.bfloat16
AF = mybir.ActivationFunctionType
ALU = mybir.AluOpType
AX = mybir.AxisListType

D = 2048
KT = 16
EPS = 1e-6
PW = 6176
ENGS = ("pe", "act", "dve", "pool", "sp")
NPOOL = 20


class Op:
    __slots__ = ("eng", "fn", "deps", "kind", "signals", "sig", "sem_i", "sem_v", "is_out")

    def __init__(self, eng, fn, kind, is_out):
        self.eng = eng
        self.fn = fn
        self.deps = []
        self.kind = kind
        self.signals = False
        self.sig = 0
        self.sem_i = 0
        self.sem_v = 0
        self.is_out = is_out


class Prog:
    def __init__(self, nc, same_eng_sync=True):
        self.nc = nc
        self.ops = {e: [] for e in ENGS}
        self.last_w = {}
        self.readers = {}
        self.same_eng_sync = same_eng_sync

    def _dep(self, op, d):
        if d is op:
            return
        if d.eng == op.eng and d.kind in ("c", "b") and op.kind in ("c", "b"):
            if op.eng == "pe" or not self.same_eng_sync or d.kind == "b":
                return
        op.deps.append(d)
        d.signals = True

    def add(self, eng, fn, reads=(), writes=(), dma=False, out=False, kind=None, acc=False):
        op = Op(eng, fn, kind or ("d" if dma else "c"), out)
        seen = set()
        deps = []
        for k in reads:
            deps.extend(self.last_w.get(k, ()))
        for k in writes:
            if not acc:
                deps.extend(self.last_w.get(k, ()))
            deps.extend(self.readers.get(k, ()))
        for d in deps:
            if id(d) in seen:
                continue
            seen.add(id(d))
            self._dep(op, d)
        for k in reads:
            self.readers.setdefault(k, []).append(op)
        for k in writes:
            if acc and not self.readers.get(k):
                self.last_w.setdefault(k, []).append(op)
            else:
                self.last_w[k] = [op]
            self.readers[k] = []
        self.ops[eng].append(op)
        return op

    def barrier(self):
        outstanding = {}
        for ws in self.last_w.values():
            for w in ws:
                outstanding[id(w)] = w
        for rs in self.readers.values():
            for r in rs:
                outstanding[id(r)] = r
        for e in ENGS:
            op = Op(e, None, "b", False)
            for d in outstanding.values():
                if d.eng == e and d.kind == "c":
                    continue
                op.deps.append(d)
                d.signals = True
            self.ops[e].append(op)
        self.last_w = {}
        self.readers = {}

    def emit(self):
        import contextlib
        nc = self.nc
        dma_count = {e: 0 for e in ENGS}
        ncc = 0
        for e in ENGS:
            c = 0
            for op in self.ops[e]:
                if op.kind == "d":
                    i = dma_count[e]
                    dma_count[e] += 1
                    op.sem_i = i % NPOOL
                    op.sem_v = 16 * (i // NPOOL + 1)
                    if op.is_out:
                        op.signals = True
                elif op.kind == "cc":
                    op.sem_i = ncc
                    ncc += 1
                elif op.kind == "c" and op.signals:
                    c += 1
                    op.sig = c
        with contextlib.ExitStack() as st:
            esem = {e: st.enter_context(nc.semaphore("s_" + e)) for e in ENGS if e != "sp"}
            dsem = {}
            for e in ENGS:
                if dma_count[e] > 0:
                    dsem[e] = [st.enter_context(nc.semaphore("d_%s_%d" % (e, i)))
                               for i in range(min(NPOOL, dma_count[e]))]
            csem = [st.enter_context(nc.semaphore("cc_%d" % i)) for i in range(ncc)]
            block = st.enter_context(nc.Block())
            outs = [op for e in ENGS for op in self.ops[e] if op.is_out]

            def sem_of(d):
                if d.kind == "d":
                    return dsem[d.eng][d.sem_i], d.sem_v
                if d.kind == "cc":
                    return csem[d.sem_i], 1
                return esem[d.eng], d.sig

            def run(e, engine):
                waited = {}

                def wait(sem, val):
                    if val <= 0 or waited.get(sem.num, 0) >= val:
                        return
                    waited[sem.num] = val
                    engine.wait_ge(sem, val)

                for op in self.ops[e]:
                    need = {}
                    for d in op.deps:
                        s, v = sem_of(d)
                        if need.get(s.num, (None, 0))[1] < v:
                            need[s.num] = (s, v)
                    for s, v in need.values():
                        wait(s, v)
                    if op.kind == "b":
                        continue
                    if op.kind == "d":
                        s = dsem[e][op.sem_i]
                        wait(s, op.sem_v - 16)
                        op.fn(engine).then_inc(s, 16)
                    elif op.kind == "cc":
                        op.fn(engine).then_inc(csem[op.sem_i])
                    else:
                        ins = op.fn(engine)
                        if op.signals:
                            ins.then_inc(esem[e], 1)
                if e == "sp":
                    for d in outs:
                        s, v = sem_of(d)
                        wait(s, v)

            names = {"pe": "tensor", "act": "scalar", "dve": "vector", "pool": "gpsimd", "sp": "sync"}
            for e in ENGS:
                if not self.ops[e] and e != "sp":
                    continue
                getattr(block, names[e])(lambda engine, e=e: run(e, engine))


class Cfg:
    def __init__(self, NC=8, NTC=2, NTL=16, DFF=5504, L=2, gather=False):
        self.NC = NC
        self.NB = NC // 2
        self.NTC = NTC
        self.NTL = NTL
        self.NT = NTC + NTL
        self.NTOK = self.NT * 128
        self.DFF = DFF
        self.FT = DFF // 128
        self.L = L
        self.gather = gather
        self.W = 6 * D // NC
        self.NCH = 2 * self.NT


C_ID, C_J, C_TRI0, C_TRI1, C_TS0, C_TS1, C_M40, C_M41, C_CIND, C_ONES = (
    0, 128, 256, 384, 512, 640, 768, 1280, 1792, 1794)
NCONST = 1794 + 128


def make_consts():
    c = np.zeros((128, NCONST), np.float32)
    i = np.arange(128)
    same = (i[:, None] // 64) == (i[None, :] // 64)
    tri0 = (same & (i[:, None] <= i[None, :])).astype(np.float32)
    tri1 = tri0.T.copy()
    c[:, C_ID:C_ID + 128] = np.eye(128)
    c[:, C_J:C_J + 128] = np.eye(128)[::-1]
    c[:, C_TRI0:C_TRI0 + 128] = tri0
    c[:, C_TRI1:C_TRI1 + 128] = tri1
    c[:, C_TS0:C_TS0 + 128] = tri1 - np.eye(128)
    c[:, C_TS1:C_TS1 + 128] = tri0 - np.eye(128)
    c[:, C_M40:C_M40 + 512] = np.tile(tri0, (1, 4))
    c[:, C_M41:C_M41 + 512] = np.tile(tri1, (1, 4))
    c[:, C_CIND] = (i < 64)
    c[:, C_CIND + 1] = (i >= 64)
    c[:, C_ONES:C_ONES + 128] = 1.0
    return c


class StopBuild(Exception):
    pass


class Bld:
    def stop(self, name):
        if getattr(self.cfg, "stop", None) == name:
            raise StopBuild(name)

    def __init__(self, cfg):
        self.cfg = cfg
        self.nc = bass.Bass("TRN2", target_bir_lowering=False)
        self.P = Prog(self.nc)
        self.sb_off = 16640
        self.sb_max = 0
        self.uid = 0
        self.ps = [self.nc.alloc_psum_tensor("psb%d" % i, [128, 512], F32) for i in range(8)]
        self.dr = {}

    def sb(self, name, shape, dt):
        n = 1
        for x in shape[1:]:
            n *= x
        nbytes = n * (4 if dt == F32 else 2)
        nbytes = (nbytes + 63) // 64 * 64
        self.uid += 1
        t = self.nc.alloc_sbuf_tensor_at("%s_%d" % (name, self.uid), list(shape), dt, offset=self.sb_off)
        self.sb_off += nbytes
        self.sb_max = max(self.sb_max, self.sb_off)
        assert self.sb_off <= 229376, ("SBUF overflow", name, self.sb_off)
        return t

    def mark(self):
        return self.sb_off

    def release(self, m):
        self.P.barrier()
        self.sb_off = m

    def dram(self, name, shape, dt, kind="Internal"):
        if kind == "Internal" and name in getattr(self.cfg, "dbg", ()):
            kind = "ExternalOutput"
        t = self.nc.dram_tensor(name, list(shape), dt, kind=kind).ap()
        self.dr[name] = t
        return t

    def dma(self, q, out, in_, reads, writes, out_flag=False):
        return self.P.add(q, lambda e: e.dma_start(out=out, in_=in_), reads, writes, dma=True, out=out_flag)

    def mm(self, out, lhsT, rhs, start, stop, reads, writes):
        return self.P.add("pe", lambda e: e.matmul(out, lhsT=lhsT, rhs=rhs, start=start, stop=stop), reads, writes)

    def tr(self, out, in_, ident, reads, writes):
        return self.P.add("pe", lambda e: e.transpose(out, in_, ident), reads, writes)

    def act(self, out, in_, func, reads, writes, scale=None, bias=None, accum=None):
        kw = {}
        if scale is not None:
            kw["scale"] = scale
        if bias is not None:
            kw["bias"] = bias
        if accum is not None:
            kw["accum_out"] = accum
        return self.P.add("act", lambda e: e.activation(out=out, in_=in_, func=func, **kw), reads, writes)

    def tt(self, out, in0, in1, op, reads, writes, eng="dve"):
        return self.P.add(eng, lambda e: e.tensor_tensor(out=out, in0=in0, in1=in1, op=op), reads, writes)

    def ts(self, out, in0, s1, s2, op0, op1, reads, writes, eng="dve"):
        if op1 is None:
            return self.P.add(eng, lambda e: e.tensor_scalar(out=out, in0=in0, scalar1=s1, scalar2=None, op0=op0), reads, writes)
        return self.P.add(eng, lambda e: e.tensor_scalar(out=out, in0=in0, scalar1=s1, scalar2=s2, op0=op0, op1=op1), reads, writes)

    def stt(self, out, in0, scalar, in1, op0, op1, reads, writes, eng="dve"):
        return self.P.add(eng, lambda e: e.scalar_tensor_tensor(out=out, in0=in0, scalar=scalar, in1=in1, op0=op0, op1=op1), reads, writes)

    def copy(self, out, in_, reads, writes, eng="dve"):
        if eng == "act":
            return self.P.add("act", lambda e: e.copy(out=out, in_=in_), reads, writes)
        return self.P.add(eng, lambda e: e.tensor_copy(out=out, in_=in_), reads, writes)


def declare_io(b):
    cfg = b.cfg
    L, NC = cfg.L, cfg.NC
    ein = lambda n, s, dt=F32: b.dram(n, s, dt, kind="ExternalInput")
    ein("x_in", [cfg.NTL * 128, D])
    ein("ctx_in", [cfg.NTC * 128, D])
    ein("consts", [128, NCONST])
    ein("cT", [128, KT, cfg.NB + 1])
    ein("sel", [cfg.NB + 1, 2])
    ein("selbc", [cfg.NB + 1, 2, 128])
    ein("maskcol", [128, NC])
    ein("w_mod_sh", [L, D, cfg.W])
    ein("b_mod_sh", [L, cfg.W])
    ein("gcol", [L, 128, 2, KT])
    ein("w_gate", [L, D, 16])
    ein("b_gate", [L, 4, 4])
    ein("wlr17", [L, 2, 17, 512])
    ein("gn_col", [L, 128, KT])
    ein("convw", [L, 128, cfg.FT, 9])
    ein("convb", [L, 128, cfg.FT])
    ein("gfin", [1, D])
    rows = D // NC if cfg.gather else D
    frows = cfg.DFF // NC if cfg.gather else cfg.DFF
    ein("w_in", [L, rows, PW - 16])
    ein("w_out", [L, rows, D])
    ein("w_up", [L, rows, 2 * cfg.DFF])
    ein("w_down", [L, frows, D])
    b.dram("y_out", [cfg.NTL * 128, D], F32, kind="ExternalOutput")


def load_consts(b):
    cfg = b.cfg
    b.cst = b.sb("consts", [128, NCONST], F32)
    b.dma("sp", b.cst[:], b.dr["consts"], [], ["consts"])
    b.mcol = b.sb("maskcol", [128, cfg.NC], F32)
    b.dma("sp", b.mcol[:], b.dr["maskcol"], [], ["maskcol"])


def cs(b, off, n=128, p0=0, p1=128):
    return b.cst[p0:p1, off:off + n]


def gather_weights(b):
    cfg = b.cfg
    L = cfg.L
    rg = [list(range(cfg.NC))]
    for name, rows, cols in (("w_in", D, PW - 16), ("w_out", D, D), ("w_up", D, 2 * cfg.DFF), ("w_down", cfg.DFF, D)):
        full = b.dram(name + "_full", [L, rows, cols], F32)
        if not cfg.gather:
            b.dr[name + "_f"] = b.dr[name]
            continue
        src = b.dr[name]
        stg = b.dram(name + "_stg", [L, rows // cfg.NC, cols], F32)
        for l in range(L):
            b.dma("sp", stg[l], src[l], [], [name + "_stg%d" % l])
            b.P.add("pool", lambda e, s=stg[l], d=full[l]: e.collective_compute(
                "AllGather", ALU.bypass, replica_groups=rg, ins=[s], outs=[d]),
                [name + "_stg%d" % l], [name + "_full%d" % l], kind="cc")
        b.dr[name + "_f"] = full


def mod_phase(b):
    cfg = b.cfg
    L, NB, W, NC = cfg.L, cfg.NB, cfg.W, cfg.NC
    R = NB + 1
    b.modcol = [b.sb("modcol%d" % l, [128, 6 * KT, 2], F32) for l in range(L)]
    b.gcolsb = b.sb("gcolsb", [128, L, 2, KT], F32)
    b.dma("sp", b.gcolsb[:], b.dr["gcol"].rearrange("l p a k -> p l a k"), [], ["gcolsb"])
    b.selsb = b.sb("selsb", [R, 2], F32)
    b.dma("sp", b.selsb[:], b.dr["sel"], [], ["selsb"])
    b.selbc = b.sb("selbc", [R, 2, 128], F32)
    b.dma("sp", b.selbc[:], b.dr["selbc"], [], ["selbc"])
    m = b.mark()
    cTf = b.sb("cTf", [128, KT, R], F32)
    cact = b.sb("cact", [128, KT, R], BF16)
    b.dma("sp", cTf[:], b.dr["cT"], [], ["cTf"])
    b.act(cact[:], cTf[:], AF.Silu, ["cTf"], ["cact"])
    modp = b.sb("modp", [R, L * W], F32)
    bmod = b.sb("bmod", [R, L * W], F32)
    b.dma("sp", bmod[:], b.dr["b_mod_sh"].rearrange("l w -> (l w)").partition_broadcast(R), [], ["bmod"])
    wb = [b.sb("wmodb%d" % i, [128, KT, 512], BF16) for i in range(2)]
    nblk = 0
    for l in range(L):
        for c0 in range(0, W, 512):
            n = min(512, W - c0)
            s = nblk % 2
            nblk += 1
            b.P.add("pool", lambda e, l=l, c0=c0, n=n, s=s: e.dma_start(
                out=wb[s][:, :, 0:n], in_=b.dr["w_mod_sh"][l, :, c0:c0 + n].rearrange("(k p) c -> p k c", p=128)),
                [], ["wmodb%d" % s], dma=True)
            ps = b.ps[s]
            for k in range(KT):
                b.mm(ps[0:R, 0:n], cact[:, k, :], wb[s][:, k, 0:n], k == 0, k == KT - 1,
                     ["cact", "wmodb%d" % s], ["ps%d" % s])
            b.tt(modp[:, l * W + c0:l * W + c0 + n], ps[0:R, 0:n], bmod[:, l * W + c0:l * W + c0 + n], ALU.add,
                 ["ps%d" % s, "bmod"], ["modp"])
    msrc = b.dram("mod_src", [R, L * W], F32)
    mdst = b.dram("mod_dst", [NC * R, L * W], F32)
    b.mdst = mdst
    b.dma("sp", msrc, modp[:], ["modp"], ["mod_src"])
    b.P.add("pool", lambda e: e.collective_compute("AllGather", ALU.bypass, replica_groups=[list(range(NC))],
                                                   ins=[msrc], outs=[mdst]), ["mod_src"], ["mod_dst"], kind="cc")
    rows = [b.sb("modrows%d" % i, [R, W], F32) for i in range(2)]
    i = 0
    for l in range(L):
        ps = b.ps[2 + l % 2]
        for r in range(NC):
            s = i % 2
            i += 1
            b.dma("sp", rows[s][:], mdst[r * R:(r + 1) * R, l * W:(l + 1) * W], ["mod_dst"], ["modrows%d" % s])
            for j in range(W // 128):
                ch = (r * W) // 128 + j
                b.mm(ps[:, 2 * ch:2 * ch + 2], rows[s][:, j * 128:(j + 1) * 128], b.selsb[:], True, True,
                     ["modrows%d" % s, "selsb"], ["ps%d" % (2 + l % 2)])
        b.copy(b.modcol[l][:].rearrange("p c r -> p (c r)"), ps[:, 0:12 * KT], ["ps%d" % (2 + l % 2)], ["modcol%d" % l])
    b.release(m)
    b.Acol = b.sb("Acol", [128, L, 2, 2, KT], F32)
    b.Bcol = b.sb("Bcol", [128, L, 2, 2, KT], F32)
    for l in range(L):
        for sub in range(2):
            for role in range(2):
                sc = b.modcol[l][:, (3 * sub + 1) * KT:(3 * sub + 2) * KT, role]
                sh = b.modcol[l][:, (3 * sub) * KT:(3 * sub + 1) * KT, role]
                b.stt(b.Acol[:, l, sub, role, :], sc, 1.0, b.gcolsb[:, l, sub, :], ALU.add, ALU.mult,
                      ["modcol%d" % l, "gcolsb"], ["Acol"])
                b.copy(b.Bcol[:, l, sub, role, :], sh, ["modcol%d" % l], ["Bcol"])


def bcast_rows(b, l, seg, role, dst, key):
    cfg = b.cfg
    R, W = cfg.NB + 1, cfg.W
    m = b.mark()
    rows = b.sb("bcrows", [R, 2048], F32)
    c0 = seg * 2048
    r0 = c0 // W
    while c0 < (seg + 1) * 2048:
        r = c0 // W
        n = min((r + 1) * W, (seg + 1) * 2048) - c0
        b.dma("sp", rows[:, c0 - seg * 2048:c0 - seg * 2048 + n],
              b.mdst[r * R:(r + 1) * R, l * W + (c0 - r * W):l * W + (c0 - r * W) + n], ["mod_dst"], ["bcrows"])
        c0 += n
    for j in range(4):
        ps = b.ps[j % 2]
        b.mm(ps[:, :], b.selbc[:, role, :], rows[:, j * 512:(j + 1) * 512], True, True, ["bcrows", "selbc"], ["ps%d" % (j % 2)])
        b.copy(dst[:, j * 512:(j + 1) * 512], ps[:, :], ["ps%d" % (j % 2)], [key], eng="act")
    b.release(m)


def norm_T(b, src_fn, src_keys, n_tiles, hT, hkey, tok0, l, sub, role_fn, reverse=False):
    m = b.mark()
    xt = [b.sb("nx%d" % i, [128, D], F32) for i in range(2)]
    junk = b.sb("njunk", [128, D], BF16)
    st = [b.sb("nst%d" % i, [128, 4], F32) for i in range(2)]
    for t in range(n_tiles):
        s = t % 2
        role = role_fn(t)
        b.dma("sp", xt[s][:], src_fn(t), [src_keys(t)], ["nx%d" % s])
        b.act(junk[:], xt[s][:], AF.Square, ["nx%d" % s], ["njunk", "nst%da" % s], accum=st[s][:, 0:1])
        b.act(st[s][:, 1:2], st[s][:, 0:1], AF.Sqrt, ["nst%da" % s], ["nst%db" % s], scale=1.0 / D, bias=EPS)
        b.P.add("dve", lambda e, s=s: e.reciprocal(out=st[s][:, 2:3], in_=st[s][:, 1:2]), ["nst%db" % s], ["nst%dc" % s])
        b.act(xt[s][:], xt[s][:], AF.Copy, ["nx%d" % s, "nst%dc" % s], ["nx%d" % s], scale=st[s][:, 2:3])
        for kb in range(4):
            ps = b.ps[4 + (t * 4 + kb) % 4]
            pk = "ps%d" % (4 + (t * 4 + kb) % 4)
            for kk in range(4):
                k = kb * 4 + kk
                if reverse:
                    b.mm(ps[:, kk * 128:(kk + 1) * 128], xt[s][:, k * 128:(k + 1) * 128], cs(b, C_J), True, True,
                         ["nx%d" % s, "consts"], [pk])
                else:
                    b.tr(ps[:, kk * 128:(kk + 1) * 128], xt[s][:, k * 128:(k + 1) * 128], cs(b, C_ID),
                         ["nx%d" % s, "consts"], [pk])
            for kk in range(4):
                k = kb * 4 + kk
                b.ts(hT[:, k, tok0 + t * 128:tok0 + (t + 1) * 128], ps[:, kk * 128:(kk + 1) * 128],
                     b.Acol[:, l, sub, role, k:k + 1], b.Bcol[:, l, sub, role, k:k + 1], ALU.mult, ALU.add,
                     [pk, "Acol", "Bcol"], [hkey])
    b.release(m)


def gemm(b, xT, xkey, kt, tok_blocks, w_ap, col0, ncols, mode, evac, cblk=512, psb=(0, 1, 2, 3), wq="pool", wkey=None):
    m = b.mark()
    wb = [b.sb("gw%d" % i, [128, kt, cblk], BF16) for i in range(2)]
    nb = 0
    cnt = 0
    for c0 in range(col0, col0 + ncols, cblk):
        n = min(cblk, col0 + ncols - c0)
        s = nb % 2
        nb += 1
        b.P.add(wq, lambda e, c0=c0, n=n, s=s: e.dma_start(
            out=wb[s][:, :, 0:n], in_=w_ap[:, c0:c0 + n].rearrange("(k p) c -> p k c", p=128)),
            ([wkey] if wkey else []), ["gw%d" % s], dma=True)
        for ti, (tok0, ntok) in enumerate(tok_blocks):
            if mode == "tok":
                pi = psb[cnt % len(psb)]
                cnt += 1
                ps, pk = b.ps[pi], "ps%d" % pi
                for k in range(kt):
                    b.mm(ps[0:ntok, 0:n], xT[:, k, tok0:tok0 + ntok], wb[s][:, k, 0:n], k == 0, k == kt - 1,
                         [xkey, "gw%d" % s], [pk])
                evac(ps[0:ntok, 0:n], pk, c0, n, ti, tok0, ntok)
            else:
                for f in range(0, n, 128):
                    nf = min(128, n - f)
                    pi = psb[cnt % len(psb)]
                    cnt += 1
                    ps, pk = b.ps[pi], "ps%d" % pi
                    for k in range(kt):
                        b.mm(ps[0:nf, 0:ntok], wb[s][:, k, f:f + nf], xT[:, k, tok0:tok0 + ntok], k == 0, k == kt - 1,
                             [xkey, "gw%d" % s], [pk])
                    evac(ps[0:nf, 0:ntok], pk, c0 + f, nf, ti, tok0, ntok)
    b.release(m)


def mlstm_prep(b, l, d, t_lo, t_hi, m_init, R):
    cfg = b.cfg
    X0, X1 = t_lo * 128, t_hi * 128
    n = X1 - X0
    c_lo, c_hi = 2 * t_lo, 2 * t_hi
    nch = c_hi - c_lo
    tpf, tcp, ta = R["tpf"], R["tcp"], R["ta"]
    sm = R["sm"]
    bg = R["bg"]
    zmg = b.dr["zmg%d" % l]
    b.dma("sp", tpf[:, X0:X1], zmg[d * 8 + 4:d * 8 + 8, X0:X1], ["zmg"], ["tpf"])
    b.dma("sp", ta[:, X0:X1], zmg[d * 8:d * 8 + 4, X0:X1], ["zmg"], ["ta"])
    b.act(tpf[:, X0:X1], tpf[:, X0:X1], AF.Exp, ["tpf", "bgn"], ["tpf"], scale=-1.0, bias=R["bgn"][:, d * 2 + 1:d * 2 + 2])
    b.act(tpf[:, X0:X1], tpf[:, X0:X1], AF.Ln, ["tpf"], ["tpf"], scale=1.0, bias=1.0)
    for c in range(c_lo, c_hi):
        b.P.add("dve", lambda e, c=c: e.tensor_tensor_scan(
            out=tcp[:, c * 64:(c + 1) * 64], data0=b.cst[0:4, C_ONES:C_ONES + 64], data1=tpf[:, c * 64:(c + 1) * 64],
            initial=0.0, op0=ALU.mult, op1=ALU.add), ["tpf", "consts"], ["tcp"])
    pf3 = tpf[:, X0:X1].rearrange("p (c j) -> p c j", j=64)
    cp3 = tcp[:, X0:X1].rearrange("p (c j) -> p c j", j=64)
    a3 = ta[:, X0:X1].rearrange("p (c j) -> p c j", j=64)
    b.P.add("dve", lambda e: e.tensor_reduce(out=sm[:, 1, c_lo:c_hi], in_=pf3, op=ALU.add, axis=AX.X), ["tpf"], ["sm1"])
    if d == 1:
        b.tt(tcp[:, X0:X1], tpf[:, X0:X1], tcp[:, X0:X1], ALU.subtract, ["tpf", "tcp"], ["tcp"])
        b.tt(cp3, cp3, sm[:, 1, c_lo:c_hi].unsqueeze(2).to_broadcast([4, nch, 64]), ALU.add, ["tcp", "sm1"], ["tcp"])
    b.stt(ta[:, X0:X1], ta[:, X0:X1], bg[:, d * 2:d * 2 + 1], tcp[:, X0:X1], ALU.add, ALU.add, ["ta", "bg", "tcp"], ["ta"])
    b.P.add("dve", lambda e: e.tensor_reduce(out=sm[:, 0, c_lo:c_hi], in_=a3, op=ALU.max, axis=AX.X), ["ta"], ["sm0"])
    order = list(range(c_lo, c_hi)) if d == 0 else list(range(c_hi - 1, c_lo - 1, -1))
    b.copy(sm[:, 4, 0:1], m_init, ["minit"], ["sm4"])
    for p, c in enumerate(order):
        b.tt(sm[:, 2, c:c + 1], sm[:, 4, p:p + 1], sm[:, 0, c:c + 1], ALU.max, ["sm4", "sm0"], ["sm2"])
        b.tt(sm[:, 4, p + 1:p + 2], sm[:, 2, c:c + 1], sm[:, 1, c:c + 1], ALU.subtract, ["sm2", "sm1"], ["sm4"])
        b.tt(sm[:, 3, c:c + 1], sm[:, 4, p:p + 1], sm[:, 2, c:c + 1], ALU.subtract, ["sm4", "sm2"], ["sm3"])
    R["mfin"] = sm[:, 4, nch:nch + 1]
    b.act(sm[:, 5, c_lo:c_hi], sm[:, 3, c_lo:c_hi], AF.Exp, ["sm3"], ["sm5"])
    Mb = sm[:, 2, c_lo:c_hi].unsqueeze(2).to_broadcast([4, nch, 64])
    b.tt(a3, a3, Mb, ALU.subtract, ["ta", "sm2"], ["ta"])
    b.act(ta[:, X0:X1], ta[:, X0:X1], AF.Exp, ["ta"], ["ta"])
    b.tt(cp3, cp3, Mb, ALU.subtract, ["tcp", "sm2"], ["tcp"])
    b.act(tcp[:, X0:X1], tcp[:, X0:X1], AF.Exp, ["tcp"], ["tcp"])
    csd = R["csd"]
    for h in range(4):
        b.ts(csd[:, h, c_lo:c_hi], sm[:, 5, c_lo:c_hi], b.cst[0:4, C_ID + h:C_ID + h + 1], None, ALU.mult, None,
             ["sm5", "consts"], ["csd"])
    for h in range(4):
        b.mm(b.ps[0][:, h * nch:(h + 1) * nch], b.cst[0:4, C_ONES:C_ONES + 128], csd[:, h, c_lo:c_hi], True, True,
             ["csd", "consts"], ["ps0"])
    for h in range(4):
        b.copy(R["cscol"][:, h, c_lo:c_hi], b.ps[0][:, h * nch:(h + 1) * nch], ["ps0"], ["cscol"])


def scan(b, l, d, tiles, St, R, out_mode, yT=None, role_fn=None):
    cfg = b.cfg
    P = b.P
    m = b.mark()
    tri = C_TRI0 if d == 0 else C_TRI1
    tst = C_TS0 if d == 0 else C_TS1
    m4 = C_M40 if d == 0 else C_M41
    bufs = {}
    for s in range(2):
        for nm in ("qg", "kg", "qm", "km"):
            bufs[nm, s] = b.sb("s%s%d" % (nm, s), [128, 4, 128], BF16)
        for nm in ("ktg", "ktm"):
            bufs[nm, s] = b.sb("s%s%d" % (nm, s), [128, 512], BF16)
        bufs["vg", s] = b.sb("svg%d" % s, [128, 4, 256], BF16)
        bufs["vm", s] = b.sb("svm%d" % s, [128, 4, 258], BF16)
        bufs["lr", s] = b.sb("slr%d" % s, [17, 128], F32)
        P.add("dve", lambda e, s=s: e.memset(bufs["vm", s][:, :, 256:258], 1.0), [], ["vm%d" % s])
        P.add("dve", lambda e, s=s: e.memset(bufs["lr", s][:], 1.0), [], ["lr%d" % s])
        if out_mode == "merge":
            bufs["o1", s] = b.sb("so1%d" % s, [128, D], F32)
            bufs["gt", s] = b.sb("sgt%d" % s, [128, D], BF16)
    if out_mode == "store":
        bufs["o1", 0] = b.sb("so10", [128, D], F32)
        bufs["o1", 1] = b.sb("so11", [128, D], F32)
    wl17 = b.sb("wl17", [17, 512], F32)
    b.dma("sp", wl17[:], b.dr["wlr17"][l, d], [], ["wl17"])
    esb = b.sb("esb", [128, 512], F32)
    psb_ = b.sb("psb", [128, 512], F32)
    wq = b.sb("wq", [128, 512], F32)
    wk = b.sb("wk", [128, 512], F32)
    wend = b.sb("wend", [128, 512], F32)
    qgt = b.sb("qgt", [128, 4, 128], BF16)
    kgt = b.sb("kgt", [128, 4, 128], BF16)
    kend = b.sb("kend", [128, 512], BF16)
    khat = b.sb("khat", [128, 512], BF16)
    sTg = b.sb("sTg", [128, 4, 128], BF16)
    sTm = b.sb("sTm", [128, 4, 128], BF16)
    gcol = b.sb("gcol", [128, 8], F32)
    ekfl = b.sb("ekfl", [128, 8], F32)
    dmx = b.sb("dmx", [128, 8], F32)
    if out_mode == "merge":
        mst = b.sb("mst", [128, 24], F32)
        mjunk = b.sb("mjunk", [128, 256], BF16)
    zq_g, zk_g, zq_m, zk_m = (b.dr[n + str(l)] for n in ("zq_g", "zk_g", "zq_m", "zk_m"))
    zkt_g, zkt_m, zv_g, zv_m, zgate, zlr = (b.dr[n + str(l)] for n in ("zkt_g", "zkt_m", "zv_g", "zv_m", "zgate", "zlr"))
    o1d = b.dr["o1_%d" % l]
    Sg, Sgb, Cm, Cmb = St["Sg"], St["Sgb"], St["Cm"], St["Cmb"]
    cscol = R["cscol"]

    def loads(i):
        t = tiles[i]
        s = i % 2
        tk = slice(t * 128, (t + 1) * 128)
        for nm, src in (("qg", zq_g), ("kg", zk_g), ("qm", zq_m), ("km", zk_m)):
            b.dma("sp", bufs[nm, s][:], src[:, :, tk].rearrange("h p t -> p h t"), ["z"], ["%s%d" % (nm, s)])
        b.dma("sp", bufs["ktg", s][:], zkt_g[tk, :], ["z"], ["ktg%d" % s])
        b.dma("sp", bufs["ktm", s][:], zkt_m[tk, :], ["z"], ["ktm%d" % s])
        b.dma("sp", bufs["vg", s][:].rearrange("p h v -> p (h v)"), zv_g[tk, :], ["z"], ["vg%d" % s])
        b.dma("sp", bufs["vm", s][:, :, 0:256], zv_m[tk, :].rearrange("p (h v) -> p h v", h=4), ["z"], ["vm%d" % s])
        b.dma("sp", bufs["lr", s][0:16, :], zlr[:, tk], ["z"], ["lr%d" % s])
        if out_mode == "merge":
            b.dma("sp", bufs["o1", s][:], o1d[tk, :], ["o1d%d" % t], ["o1%d" % s])
            b.dma("sp", bufs["gt", s][:], zgate[tk, :], ["z"], ["gt%d" % s])

    loads(0)
    first = True
    for i, t in enumerate(tiles):
        s = i % 2
        if i + 1 < len(tiles):
            loads(i + 1)
        corder = [0, 1] if d == 0 else [1, 0]
        qg, kg, qm, km = (bufs[nm, s] for nm in ("qg", "kg", "qm", "km"))
        ktg, ktm, vg, vm, lr = (bufs[nm, s] for nm in ("ktg", "ktm", "vg", "vm", "lr"))
        K = lambda nm: "%s%d" % (nm, s)
        want_out = out_mode is not None and (role_fn is None or role_fn(t))
        b.mm(b.ps[0][:, :], lr[:, :], wl17[:, :], True, True, [K("lr"), "wl17"], ["ps0"])
        b.act(esb[:], b.ps[0][:, :], AF.Exp, ["ps0"], ["esb"], scale=-1.0)
        b.act(psb_[:], esb[:], AF.Ln, ["esb"], ["psb"], scale=1.0, bias=1.0)
        for h in range(4):
            b.mm(b.ps[1][:, h * 128:(h + 1) * 128], psb_[:, h * 128:(h + 1) * 128], cs(b, tri), True, True,
                 ["psb", "consts"], ["ps1"])
        b.act(wq[:], b.ps[1][:, :], AF.Exp, ["ps1"], ["wq"], scale=-1.0 / 16)
        b.act(wk[:], b.ps[1][:, :], AF.Exp, ["ps1"], ["wk"], scale=1.0 / 16)
        b.tt(qgt[:].rearrange("p h t -> p (h t)"), qg[:].rearrange("p h t -> p (h t)"), wq[:], ALU.mult, [K("qg"), "wq"], ["qgt"])
        b.tt(kgt[:].rearrange("p h t -> p (h t)"), kg[:].rearrange("p h t -> p (h t)"), wk[:], ALU.mult, [K("kg"), "wk"], ["kgt"])
        b.mm(b.ps[0][:, :], cs(b, tst), psb_[:, :], True, True, ["psb", "consts"], ["ps0"])
        b.act(wend[:], b.ps[0][:, :], AF.Exp, ["ps0"], ["wend"], scale=-1.0 / 16)
        b.tt(kend[:], ktg[:], wend[:], ALU.mult, [K("ktg"), "wend"], ["kend"])
        for h in range(4):
            b.mm(b.ps[1][:, 2 * h:2 * h + 2], psb_[:, h * 128:(h + 1) * 128], b.cst[:, C_CIND:C_CIND + 2], True, True,
                 ["psb", "consts"], ["ps1"])
        b.act(gcol[:], b.ps[1][:, 0:8], AF.Exp, ["ps1"], ["gcol"], scale=-1.0 / 16)
        b.mm(b.ps[0][:, 0:4], R["ta"][:, t * 128:(t + 1) * 128], b.cst[0:4, C_ID:C_ID + 4], True, True, ["ta", "consts"], ["ps0"])
        b.mm(b.ps[0][:, 4:8], R["tcp"][:, t * 128:(t + 1) * 128], b.cst[0:4, C_ID:C_ID + 4], True, True, ["tcp", "consts"], ["ps0"])
        b.copy(ekfl[:], b.ps[0][:, 0:8], ["ps0"], ["ekfl"])
        if want_out:
            for h in range(4):
                b.mm(b.ps[2][:, h * 128:(h + 1) * 128], kgt[:, h, :], qgt[:, h, :], True, True, ["kgt", "qgt"], ["ps2"])
            b.tt(sTg[:].rearrange("p h t -> p (h t)"), b.ps[2][:, :], cs(b, m4, 512), ALU.mult, ["ps2", "consts"], ["sTg"])
            for h in range(4):
                b.mm(b.ps[2][:, h * 128:(h + 1) * 128], km[:, h, :], qm[:, h, :], True, True, [K("km"), K("qm")], ["ps2"])
            for h in range(4):
                b.stt(sTm[:, h, :], b.ps[2][:, h * 128:(h + 1) * 128], ekfl[:, h:h + 1], cs(b, tri), ALU.mult, ALU.mult,
                      ["ps2", "ekfl", "consts"], ["sTm"])
        for h in range(4):
            b.ts(khat[:, h * 128:(h + 1) * 128], ktm[:, h * 128:(h + 1) * 128], ekfl[:, h:h + 1], None, ALU.mult, None,
                 [K("ktm"), "ekfl"], ["khat"])
        o1t = bufs["o1", s] if out_mode in ("merge", "store") else None
        for h in range(4):
            po = b.ps[3 + h // 2]
            pok = "ps%d" % (3 + h // 2)
            hv = slice((h % 2) * 256, (h % 2) * 256 + 256)
            if want_out:
                b.mm(po[:, hv], sTg[:, h, :], vg[:, h, :], True, False, ["sTg", K("vg")], [pok])
            for ci, c in enumerate(corder):
                cp = slice(c * 64, (c + 1) * 64)
                if want_out:
                    b.mm(po[cp, hv], qgt[:, h, cp], Sgb[:, h, :], False, ci == 1, ["qgt", "Sgb%d" % h], [pok])
                pkv = b.ps[7]
                kvs = slice(((h * 2 + ci) % 2) * 256, ((h * 2 + ci) % 2) * 256 + 256)
                b.mm(pkv[:, kvs], kend[cp, h * 128:(h + 1) * 128], vg[cp, h, :], True, True, ["kend", K("vg")], ["ps7_%d" % ((h * 2 + ci) % 2)])
                b.stt(Sg[:, h, :], Sg[:, h, :], gcol[:, 2 * h + c:2 * h + c + 1], pkv[:, kvs], ALU.mult, ALU.add,
                      ["Sg%d" % h, "gcol", "ps7_%d" % ((h * 2 + ci) % 2)], ["Sg%d" % h])
                b.copy(Sgb[:, h, :], Sg[:, h, :], ["Sg%d" % h], ["Sgb%d" % h], eng="act")
            if want_out:
                if out_mode == "store":
                    b.copy(o1t[:, h * 256:(h + 1) * 256], po[:, hv], [pok], [K("o1")], eng="act")
                else:
                    b.tt(o1t[:, h * 256:(h + 1) * 256], po[:, hv], o1t[:, h * 256:(h + 1) * 256], ALU.add, [pok, K("o1")], [K("o1")])
        for h in range(4):
            po = b.ps[3 + h]
            pok = "ps%d" % (3 + h)
            if want_out:
                b.mm(po[:, 0:257], sTm[:, h, :], vm[:, h, 0:257], True, False, ["sTm", K("vm")], [pok])
            for ci, c in enumerate(corder):
                cp = slice(c * 64, (c + 1) * 64)
                cidx = 2 * t + c
                if first and h == 0 and ci == 0:
                    pass
                if want_out:
                    b.mm(po[cp, 0:257], qm[:, h, cp], Cmb[:, h, 0:257], False, ci == 1, [K("qm"), "Cmb%d" % h], [pok])
                pkv = b.ps[7]
                b.mm(pkv[:, 0:257], khat[cp, h * 128:(h + 1) * 128], vm[cp, h, 0:257], True, True, ["khat", K("vm")], ["ps7_0", "ps7_1"])
                b.stt(Cm[:, h, 0:257], Cm[:, h, 0:257], cscol[:, h, cidx:cidx + 1], pkv[:, 0:257], ALU.mult, ALU.add,
                      ["Cm%d" % h, "cscol", "ps7_0", "ps7_1"], ["Cm%d" % h])
                if ci == 0:
                    nidx = 2 * t + corder[1]
                elif i + 1 < len(tiles):
                    nidx = 2 * tiles[i + 1] + corder[0]
                else:
                    nidx = None
                if nidx is not None:
                    b.act(Cmb[:, h, 0:257], Cm[:, h, 0:257], AF.Copy, ["Cm%d" % h, "cscol"], ["Cmb%d" % h], scale=cscol[:, h, nidx:nidx + 1])
            if want_out:
                b.tt(dmx[:, h:h + 1], po[:, 256:257], ekfl[:, 4 + h:5 + h], ALU.max, [pok, "ekfl"], ["dmx"])
                b.stt(dmx[:, h:h + 1], po[:, 256:257], -1.0, dmx[:, h:h + 1], ALU.mult, ALU.max, [pok, "dmx"], ["dmx"])
                b.P.add("dve", lambda e, h=h: e.reciprocal(out=dmx[:, 4 + h:5 + h], in_=dmx[:, h:h + 1]), ["dmx"], ["dmx"])
                oc = slice(1024 + h * 256, 1024 + (h + 1) * 256)
                if out_mode == "store":
                    b.act(o1t[:, oc], po[:, 0:256], AF.Copy, [pok, "dmx"], [K("o1")], scale=dmx[:, 4 + h:5 + h])
                else:
                    b.stt(o1t[:, oc], po[:, 0:256], dmx[:, 4 + h:5 + h], o1t[:, oc], ALU.mult, ALU.add, [pok, "dmx", K("o1")], [K("o1")])
        first = False
        if want_out and out_mode == "store":
            b.dma("sp", o1d[t * 128:(t + 1) * 128, :], o1t[:], [K("o1")], ["o1d%d" % t])
        if want_out and out_mode == "merge":
            gt = bufs["gt", s]
            for h in range(8):
                b.act(mjunk[:], o1t[:, h * 256:(h + 1) * 256], AF.Square, [K("o1")], ["mjunk", "mst"], accum=mst[:, h:h + 1])
            b.act(mst[:, 8:16], mst[:, 0:8], AF.Sqrt, ["mst"], ["mst"], scale=1.0 / 256, bias=EPS)
            b.P.add("dve", lambda e: e.reciprocal(out=mst[:, 16:24], in_=mst[:, 8:16]), ["mst"], ["mst"])
            b.act(gt[:, 0:1024], gt[:, 0:1024], AF.Silu, [K("gt")], [K("gt")])
            b.act(gt[:, 1024:2048], gt[:, 1024:2048], AF.Sigmoid, [K("gt")], [K("gt")])
            for h in range(8):
                b.act(o1t[:, h * 256:(h + 1) * 256], o1t[:, h * 256:(h + 1) * 256], AF.Copy, [K("o1"), "mst"], [K("o1")],
                      scale=mst[:, 16 + h:17 + h])
            b.tt(o1t[:], o1t[:], gt[:], ALU.mult, [K("o1"), K("gt")], [K("o1")])
            for kb in range(4):
                pi = 3 + kb
                ps, pk = b.ps[pi], "ps%d" % pi
                for kk in range(4):
                    k = kb * 4 + kk
                    b.tr(ps[:, kk * 128:(kk + 1) * 128], o1t[:, k * 128:(k + 1) * 128], cs(b, C_ID), [K("o1"), "consts"], [pk])
                for kk in range(4):
                    k = kb * 4 + kk
                    b.act(yT[:, k, t * 128:(t + 1) * 128], ps[:, kk * 128:(kk + 1) * 128], AF.Copy, [pk, "gncol"], ["yT"],
                          scale=b.gncol[:, l, k:k + 1])
    b.release(m)


class Evac:
    def __init__(self, b, name, width, dt, n=3):
        self.b = b
        self.name = name
        self.st = [b.sb("%s%d" % (name, i), [128, width], dt) for i in range(n)]
        self.i = 0

    def go(self, ps, pk, npart, ncol, dst, dkey, scale=None, acc=True):
        b = self.b
        s = self.i % len(self.st)
        self.i += 1
        k = "%s%d" % (self.name, s)
        stg = self.st[s][0:npart, 0:ncol]
        if self.i % 2 == 0:
            b.act(stg, ps, AF.Copy, [pk], [k], scale=(1.0 if scale is None else scale))
        elif scale is None:
            b.copy(stg, ps, [pk], [k])
        else:
            b.ts(stg, ps, scale, None, ALU.mult, None, [pk], [k])
        b.P.add("sp", lambda e: e.dma_start(out=dst, in_=stg), [k], [dkey], dma=True, acc=acc)


def tok_tiles(t0, t1):
    return [(t * 128, 128) for t in range(t0, t1)]


def tok_blocks512(n0, n1):
    out = []
    while n0 < n1:
        n = min(512, n1 - n0)
        out.append((n0, n))
        n0 += n
    return out


def mixer(b, l, xsrc, xkey, csrc, ckey, xdst, xdkey, cdst, cdkey):
    cfg = b.cfg
    NT, NTC, NTL, NTOK, NCH = cfg.NT, cfg.NTC, cfg.NTL, cfg.NTOK, cfg.NCH
    upd_ctx = True
    dk = 128 ** -0.5
    for nm in ("zq_g", "zk_g", "zq_m", "zk_m"):
        b.dram("%s%d" % (nm, l), [4, 128, NTOK], BF16)
    for nm, w in (("zkt_g", 512), ("zkt_m", 512), ("zv_g", 1024), ("zv_m", 1024), ("zgate", 2048)):
        b.dram("%s%d" % (nm, l), [NTOK, w], BF16)
    b.dram("zlr%d" % l, [16, NTOK], F32)
    b.dram("zmg%d" % l, [16, NTOK], F32)
    b.dram("o1_%d" % l, [NTOK, D], F32)
    st_src = b.dram("st_src%d" % l, [128, 2056], F32)
    st_dst = b.dram("st_dst%d" % l, [cfg.NC * 128, 2056], F32)
    m0 = b.mark()
    hT = b.sb("hT", [128, KT, NTOK], BF16)
    norm_T(b, lambda t: csrc[t * 128:(t + 1) * 128, :], lambda t: "%s%d" % (ckey, t), NTC, hT, "hT", 0, l, 0, lambda t: 1)
    norm_T(b, lambda t: xsrc[t * 128:(t + 1) * 128, :], lambda t: "%s%d" % (xkey, t), NTL, hT, "hT", NTC * 128, l, 0, lambda t: 0)
    b.stop("A%d" % l)
    m1 = b.mark()
    evb = Evac(b, "evb", 512, BF16)
    evf = Evac(b, "evf", 512, F32, n=2)
    w_in = b.dr["w_in_f"][l]
    tt_all = tok_tiles(0, NT)
    tb_all = tok_blocks512(0, NTOK)

    def tok_group(c_lo, width, dst, dcol0, scale=None):
        def ev(ps, pk, c0, n, ti, tok0, ntok):
            evb.go(ps, pk, ntok, n, dst[tok0:tok0 + ntok, dcol0 + c0 - c_lo:dcol0 + c0 - c_lo + n], "z", scale)
        gemm(b, hT, "hT", KT, tt_all, w_in, c_lo, width, "tok", ev, wkey="w_in_full%d" % l)

    def feat_group(c_lo, width, dst, scale=None):
        def ev(ps, pk, c0, nf, ti, tok0, ntok):
            evb.go(ps, pk, nf, ntok, dst[(c0 - c_lo) // 128, :, tok0:tok0 + ntok], "z", scale)
        gemm(b, hT, "hT", KT, tb_all, w_in, c_lo, width, "feat", ev, wkey="w_in_full%d" % l)

    Z = lambda nm: b.dr["%s%d" % (nm, l)]
    feat_group(0, 512, Z("zq_g"), dk)
    feat_group(512, 512, Z("zk_g"))
    feat_group(3088, 512, Z("zq_m"))
    feat_group(3600, 512, Z("zk_m"), dk)

    def ev_lr(ps, pk, c0, nf, ti, tok0, ntok):
        evf.go(ps, pk, 16, ntok, Z("zlr")[:, tok0:tok0 + ntok], "z")
    gemm(b, hT, "hT", KT, tb_all, w_in, 3072, 16, "feat", ev_lr, wkey="w_in_full%d" % l)

    def ev_mg(ps, pk, c0, nf, ti, tok0, ntok):
        evf.go(ps, pk, 16, ntok, Z("zmg")[:, tok0:tok0 + ntok], "zmg")
    gemm(b, hT, "hT", KT, tb_all, b.dr["w_gate"][l], 0, 16, "feat", ev_mg)
    tok_group(512, 512, Z("zkt_g"), 0)
    tok_group(1024, 1024, Z("zv_g"), 0)
    tok_group(2048, 1024, Z("zgate"), 0)
    tok_group(3600, 512, Z("zkt_m"), 0, dk)
    tok_group(4112, 1024, Z("zv_m"), 0)
    tok_group(5136, 1024, Z("zgate"), 1024)
    b.release(m0)
    b.stop("G%d" % l)
    St = {"Sg": b.sb("Sg", [128, 4, 256], F32), "Sgb": b.sb("Sgb", [128, 4, 256], BF16),
          "Cm": b.sb("Cm", [128, 4, 258], F32), "Cmb": b.sb("Cmb", [128, 4, 258], BF16)}
    R = {"tpf": b.sb("tpf", [4, NTOK], F32), "tcp": b.sb("tcp", [4, NTOK], F32), "ta": b.sb("ta", [4, NTOK], F32),
         "sm": b.sb("sm", [4, 8, NCH + 1], F32), "bg": b.sb("bg", [4, 4], F32), "bgn": b.sb("bgn", [4, 4], F32),
         "csd": b.sb("csd", [4, 4, NCH], F32), "cscol": b.sb("cscol", [128, 4, NCH], F32)}
    minit = b.sb("minit", [4, 4], F32)
    mz = b.sb("mz", [128, 4], F32)
    b.dma("sp", R["bg"][:], b.dr["b_gate"][l], [], ["bg"])
    b.ts(R["bgn"][:], R["bg"][:], -1.0, None, ALU.mult, None, ["bg"], ["bgn"])

    def zero_state():
        for h in range(4):
            b.P.add("dve", lambda e, h=h: e.memset(St["Sg"][:, h, :], 0.0), [], ["Sg%d" % h])
            b.P.add("dve", lambda e, h=h: e.memset(St["Cm"][:, h, :], 0.0), [], ["Cm%d" % h])
        b.P.add("dve", lambda e: e.memset(minit[:], 0.0), [], ["minit"])

    def prime(first_chunk):
        for h in range(4):
            b.copy(St["Sgb"][:, h, :], St["Sg"][:, h, :], ["Sg%d" % h], ["Sgb%d" % h], eng="act")
            b.act(St["Cmb"][:, h, 0:257], St["Cm"][:, h, 0:257], AF.Copy, ["Cm%d" % h, "cscol"], ["Cmb%d" % h],
                  scale=R["cscol"][:, h, first_chunk:first_chunk + 1])

    zero_state()
    mlstm_prep(b, l, 0, 0, NT, minit[:, 0:1], R)
    prime(0)
    scan(b, l, 0, list(range(NT)), St, R, "store", role_fn=(None if upd_ctx else (lambda t: t >= NTC)))
    b.stop("S1_%d" % l)
    b.P.add("dve", lambda e: e.memset(mz[:], 0.0), [], ["mz"])
    b.copy(mz[0:4, 0:1], R["mfin"], ["sm4"], ["mz"])
    b.dma("sp", st_src[:, 0:1024], St["Sg"][:].rearrange("p h v -> p (h v)"), ["Sg%d" % h for h in range(4)], ["st_src"], )
    b.P.add("sp", lambda e: e.dma_start(out=st_src[:, 1024:2052].rearrange("p (h v) -> p h v", h=4), in_=St["Cm"][:, :, 0:257]),
            ["Cm%d" % h for h in range(4)], ["st_src"], dma=True, acc=True)
    b.P.add("sp", lambda e: e.dma_start(out=st_src[:, 2052:2056], in_=mz[:]), ["mz"], ["st_src"], dma=True, acc=True)
    b.P.add("pool", lambda e: e.collective_compute("AllGather", ALU.bypass, replica_groups=[list(range(cfg.NC))],
                                                   ins=[st_src], outs=[st_dst]), ["st_src"], ["st_dst"], kind="cc")
    yT = b.sb("yT", [128, KT, NTOK], BF16)
    b.stop("X_%d" % l)
    if upd_ctx:
        zero_state()
        mlstm_prep(b, l, 1, 0, NTC, minit[:, 0:1], R)
        prime(2 * NTC - 1)
        scan(b, l, 1, list(range(NTC - 1, -1, -1)), St, R, "merge", yT=yT)
    b.stop("S0_%d" % l)
    m2 = b.mark()
    blk = [b.sb("blk%d" % i, [128, 2056], F32) for i in range(2)]
    Sgf = St["Sg"][:].rearrange("p h v -> p (h v)")
    allS = ["Sg%d" % h for h in range(4)]
    allC = ["Cm%d" % h for h in range(4)]
    for r in range(cfg.NC):
        s = r % 2
        b.dma("sp", blk[s][:], st_dst[r * 128:(r + 1) * 128, :], ["st_dst"], ["blk%d" % s])
        mc = b.mcol[:, r:r + 1]
        cv = blk[s][:, 1024:2052].rearrange("p (h v) -> p h v", h=4)
        if r == 0:
            b.ts(Sgf, blk[s][:, 0:1024], mc, None, ALU.mult, None, ["blk%d" % s, "maskcol"], allS)
            b.ts(St["Cm"][:, :, 0:257], cv, mc, None, ALU.mult, None, ["blk%d" % s, "maskcol"], allC)
            b.ts(minit[:, 0:1], blk[s][0:4, 2052:2053], b.mcol[0:4, r:r + 1], None, ALU.mult, None, ["blk%d" % s, "maskcol"], ["minit"])
        else:
            b.stt(Sgf, blk[s][:, 0:1024], mc, Sgf, ALU.mult, ALU.add, ["blk%d" % s, "maskcol"] + allS, allS)
            b.stt(St["Cm"][:, :, 0:257], cv, mc, St["Cm"][:, :, 0:257], ALU.mult, ALU.add, ["blk%d" % s, "maskcol"] + allC, allC)
            b.stt(minit[:, 0:1], blk[s][0:4, 2052:2053], b.mcol[0:4, r:r + 1], minit[:, 0:1], ALU.mult, ALU.add,
                  ["blk%d" % s, "maskcol", "minit"], ["minit"])
    b.release(m2)
    b.stop("XS_%d" % l)
    mlstm_prep(b, l, 1, NTC, NT, minit[:, 0:1], R)
    prime(2 * NT - 1)
    scan(b, l, 1, list(range(NT - 1, NTC - 1, -1)), St, R, "merge", yT=yT)
    b.stop("S2_%d" % l)
    m3 = b.mark()
    gt1 = b.sb("gt1", [128, D], F32)
    bcast_rows(b, l, 2, 0, gt1, "gt1")
    gt1c = None
    if upd_ctx:
        gt1c = b.sb("gt1c", [128, D], F32)
        bcast_rows(b, l, 2, 1, gt1c, "gt1c")
    xs = [b.sb("oxs%d" % i, [128, 512], F32) for i in range(3)]
    xt2 = [b.sb("oxt%d" % i, [128, 512], F32) for i in range(3)]
    cnt = [0]

    def ev_out(ps, pk, c0, n, ti, tok0, ntok):
        t = tok0 // 128
        s = cnt[0] % 3
        cnt[0] += 1
        if t < NTC:
            src, skey, dst, dkey, g, gk = csrc, ckey, cdst, cdkey, gt1c, "gt1c"
            tl = t
        else:
            src, skey, dst, dkey, g, gk = xsrc, xkey, xdst, xdkey, gt1, "gt1"
            tl = t - NTC
        b.dma("sp", xs[s][:, 0:n], src[tl * 128:(tl + 1) * 128, c0:c0 + n], ["%s%d" % (skey, tl)], ["oxs%d" % s])
        b.tt(xt2[s][:, 0:n], ps, g[:, c0:c0 + n], ALU.mult, [pk, gk], ["oxt%d" % s])
        b.tt(xs[s][:, 0:n], xt2[s][:, 0:n], xs[s][:, 0:n], ALU.add, ["oxt%d" % s, "oxs%d" % s], ["oxs%d" % s])
        b.P.add("sp", lambda e: e.dma_start(out=dst[tl * 128:(tl + 1) * 128, c0:c0 + n], in_=xs[s][:, 0:n]),
                ["oxs%d" % s], ["%s%d" % (dkey, tl)], dma=True, acc=True)
    gemm(b, yT, "yT", KT, tok_tiles(0 if upd_ctx else NTC, NT), b.dr["w_out_f"][l], 0, D, "tok", ev_out, wkey="w_out_full%d" % l)
    b.release(m0)


def ffn(b, l, src, skey, dst, dkey, role, ntok, rows, width, halo, aT_d):
    cfg = b.cfg
    FT, DFF = cfg.FT, cfg.DFF
    nt = ntok // 128
    next_ = ntok + (128 if halo is not None else 0)
    ngate = ntok + (width if halo is not None else 0)
    w_up = b.dr["w_up_f"][l]
    w_dn = b.dr["w_down_f"][l]
    m0 = b.mark()
    h2T = b.sb("h2T", [128, KT, next_], BF16)
    norm_T(b, lambda t: src[t * 128:(t + 1) * 128, :], lambda t: "%s%d" % (skey, t), nt, h2T, "h2T", 0, l, 1, lambda t: role)
    if halo is not None:
        norm_T(b, lambda t: halo, lambda t: "halo", 1, h2T, "h2T", ntok, l, 1, lambda t: role, reverse=True)
    cw = b.sb("cw", [128, FT, 9], F32)
    cb = b.sb("cb", [128, FT], F32)
    b.dma("sp", cw[:], b.dr["convw"][l], [], ["cw"])
    b.dma("sp", cb[:], b.dr["convb"][l], [], ["cb"])
    NF = 2
    ug = [b.sb("ug%d" % i, [128, rows + 2, width], F32) for i in range(NF)]
    acc = [b.sb("acc%d" % i, [128, rows, width], F32) for i in range(NF)]
    sg = [b.sb("sg%d" % i, [128, ntok], BF16) for i in range(NF)]
    ast = [b.sb("ast%d" % i, [128, ntok], BF16) for i in range(NF)]
    wbg = [b.sb("wbg%d" % i, [128, KT, 128 * NF], BF16) for i in range(2)]
    wbv = [b.sb("wbv%d" % i, [128, KT, 128 * NF], BF16) for i in range(2)]
    for i in range(NF):
        b.P.add("dve", lambda e, i=i: e.memset(ug[i][:], 0.0), [], ["ug%d" % i])
    gblocks = tok_blocks512(0, ngate)
    vblocks = tok_blocks512(0, ntok)
    blk = 0
    for j0 in range(0, FT, NF):
        nfj = min(NF, FT - j0)
        s = blk % 2
        blk += 1
        b.P.add("pool", lambda e, j0=j0, nfj=nfj, s=s: e.dma_start(
            out=wbg[s][:, :, 0:128 * nfj], in_=w_up[:, j0 * 128:(j0 + nfj) * 128].rearrange("(k p) c -> p k c", p=128)),
            ["w_up_full%d" % l], ["wbg%d" % s], dma=True)
        b.P.add("pool", lambda e, j0=j0, nfj=nfj, s=s: e.dma_start(
            out=wbv[s][:, :, 0:128 * nfj], in_=w_up[:, DFF + j0 * 128:DFF + (j0 + nfj) * 128].rearrange("(k p) c -> p k c", p=128)),
            ["w_up_full%d" % l], ["wbv%d" % s], dma=True)
        cnt = 0
        for jj in range(nfj):
            j = j0 + jj
            ugf = ug[jj][:].rearrange("p r w -> p (r w)")
            for (tok0, n) in gblocks:
                pi = cnt % 4
                cnt += 1
                ps, pk = b.ps[pi], "ps%d" % pi
                for k in range(KT):
                    b.mm(ps[:, 0:n], wbg[s][:, k, jj * 128:(jj + 1) * 128], h2T[:, k, tok0:tok0 + n], k == 0, k == KT - 1,
                         ["h2T", "wbg%d" % s], [pk])
                b.copy(ugf[:, width + tok0:width + tok0 + n], ps[:, 0:n], [pk], ["ug%d" % jj], eng="act")
            a_ = acc[jj]
            b.ts(a_[:], ug[jj][:, 1:rows + 1, :], cw[:, j, 4:5], None, ALU.mult, None, ["ug%d" % jj, "cw"], ["acc%d" % jj])
            for dy in (-1, 0, 1):
                for dx in (-1, 0, 1):
                    if dy == 0 and dx == 0:
                        continue
                    c0, c1 = max(0, -dx), width - max(0, dx)
                    tap = (dy + 1) * 3 + (dx + 1)
                    b.stt(a_[:, :, c0:c1], ug[jj][:, 1 + dy:1 + dy + rows, c0 + dx:c1 + dx], cw[:, j, tap:tap + 1], a_[:, :, c0:c1],
                          ALU.mult, ALU.add, ["ug%d" % jj, "cw", "acc%d" % jj], ["acc%d" % jj])
            b.act(sg[jj][:], a_[:].rearrange("p r w -> p (r w)"), AF.Silu, ["acc%d" % jj, "cb"], ["sg%d" % jj], bias=cb[:, j:j + 1])
            for (tok0, n) in vblocks:
                pi = 4 + cnt % 4
                cnt += 1
                ps, pk = b.ps[pi], "ps%d" % pi
                for k in range(KT):
                    b.mm(ps[:, 0:n], wbv[s][:, k, jj * 128:(jj + 1) * 128], h2T[:, k, tok0:tok0 + n], k == 0, k == KT - 1,
                         ["h2T", "wbv%d" % s], [pk])
                b.tt(ast[jj][:, tok0:tok0 + n], ps[:, 0:n], sg[jj][:, tok0:tok0 + n], ALU.mult, [pk, "sg%d" % jj], ["ast%d" % jj])
            b.P.add("sp", lambda e, j=j, jj=jj: e.dma_start(out=aT_d[j * 128:(j + 1) * 128, 0:ntok], in_=ast[jj][:]),
                    ["ast%d" % jj], ["aT_d"], dma=True, acc=True)
    b.release(m0)
    b.stop("F1_%d_%d" % (l, role))
    gt2 = b.sb("gt2", [128, D], F32)
    bcast_rows(b, l, 5, role, gt2, "gt2")
    b.stop("F2a")
    half = min(ntok, 1024)
    aT = b.sb("aTs", [128, FT, half], BF16)
    xs = [b.sb("fxs%d" % i, [128, 512], F32) for i in range(3)]
    xt2 = [b.sb("fxt%d" % i, [128, 512], F32) for i in range(3)]
    cnt2 = [0]
    for h0 in range(0, ntok, half):
        for j in range(FT):
            b.P.add("sp", lambda e, j=j, h0=h0: e.dma_start(out=aT[:, j, :], in_=aT_d[j * 128:(j + 1) * 128, h0:h0 + half]),
                    ["aT_d"], ["aTs"], dma=True, acc=True)

        def ev(ps, pk, c0, n, ti, tok0, ntk, h0=h0):
            t = (h0 + tok0) // 128
            s = cnt2[0] % 3
            cnt2[0] += 1
            b.dma("sp", xs[s][:, 0:n], src[t * 128:(t + 1) * 128, c0:c0 + n], ["%s%d" % (skey, t)], ["fxs%d" % s])
            b.tt(xt2[s][:, 0:n], ps, gt2[:, c0:c0 + n], ALU.mult, [pk, "gt2"], ["fxt%d" % s])
            b.tt(xs[s][:, 0:n], xt2[s][:, 0:n], xs[s][:, 0:n], ALU.add, ["fxt%d" % s, "fxs%d" % s], ["fxs%d" % s])
            b.P.add("sp", lambda e: e.dma_start(out=dst[t * 128:(t + 1) * 128, c0:c0 + n], in_=xs[s][:, 0:n]),
                    ["fxs%d" % s], ["%s%d" % (dkey, t)], dma=True, acc=True)
        gemm(b, aT, "aTs", FT, tok_tiles(0, half // 128), w_dn, 0, D, "tok", ev, cblk=512, wkey="w_down_full%d" % l)
    b.release(m0)


def final_norm(b, src, skey, nt):
    m0 = b.mark()
    gf = b.sb("gf", [128, D], F32)
    b.dma("sp", gf[:], b.dr["gfin"][0].partition_broadcast(128), [], ["gf"])
    xt = [b.sb("fx%d" % i, [128, D], F32) for i in range(2)]
    junk = b.sb("fjunk", [128, D], BF16)
    st = [b.sb("fst%d" % i, [128, 4], F32) for i in range(2)]
    for t in range(nt):
        s = t % 2
        b.dma("sp", xt[s][:], src[t * 128:(t + 1) * 128, :], ["%s%d" % (skey, t)], ["fx%d" % s])
        b.act(junk[:], xt[s][:], AF.Square, ["fx%d" % s], ["fjunk", "fst%d" % s], accum=st[s][:, 0:1])
        b.act(st[s][:, 1:2], st[s][:, 0:1], AF.Sqrt, ["fst%d" % s], ["fst%d" % s], scale=1.0 / D, bias=EPS)
        b.P.add("dve", lambda e, s=s: e.reciprocal(out=st[s][:, 2:3], in_=st[s][:, 1:2]), ["fst%d" % s], ["fst%d" % s])
        b.stt(xt[s][:], xt[s][:], st[s][:, 2:3], gf[:], ALU.mult, ALU.mult, ["fx%d" % s, "fst%d" % s, "gf"], ["fx%d" % s])
        b.dma("sp", b.dr["y_out"][t * 128:(t + 1) * 128, :], xt[s][:], ["fx%d" % s], ["y%d" % t], out_flag=True)
    b.release(m0)


def halo_exchange(b, l, xm):
    cfg = b.cfg
    NTL, NC = cfg.NTL, cfg.NC
    hsrc = xm[(NTL - 1) * 128:NTL * 128, :]
    hdst = b.dram("hal_dst%d" % l, [NC * 128, D], F32)
    hx = b.dram("halo_x%d" % l, [128, D], F32)
    b.P.add("pool", lambda e: e.collective_compute("AllGather", ALU.bypass, replica_groups=[list(range(NC))],
                                                   ins=[hsrc], outs=[hdst]), ["XM%d_%d" % (l, NTL - 1)], ["hal_dst"], kind="cc")
    m0 = b.mark()
    accx = b.sb("hacc", [128, D], F32)
    blk = [b.sb("hblk%d" % i, [128, D], F32) for i in range(2)]
    for r in range(NC):
        s = r % 2
        b.dma("sp", blk[s][:], hdst[r * 128:(r + 1) * 128, :], ["hal_dst"], ["hblk%d" % s])
        if r == 0:
            b.ts(accx[:], blk[s][:], b.mcol[:, r:r + 1], None, ALU.mult, None, ["hblk%d" % s, "maskcol"], ["hacc"])
        else:
            b.stt(accx[:], blk[s][:], b.mcol[:, r:r + 1], accx[:], ALU.mult, ALU.add, ["hblk%d" % s, "maskcol", "hacc"], ["hacc"])
    b.dma("sp", hx, accx[:], ["hacc"], ["halo"])
    b.release(m0)
    return hx


def build(cfg):
    b = Bld(cfg)
    declare_io(b)
    try:
        build_body(b)
    except StopBuild:
        pass
    b.P.emit()
    return b.nc


def build_body(b):
    cfg = b.cfg
    L, NTL, NTC = cfg.L, cfg.NTL, cfg.NTC
    load_consts(b)
    b.stop("consts")
    gather_weights(b)
    b.stop("gather")
    mod_phase(b)
    b.stop("mod")
    b.gncol = b.sb("gncol", [128, L, KT], F32)
    b.dma("sp", b.gncol[:], b.dr["gn_col"].rearrange("l p k -> p l k"), [], ["gncol"])
    x, xk = b.dr["x_in"], "XIN"
    c, ck = b.dr["ctx_in"], "CIN"
    for l in range(L):
        upd = l < L - 1
        xm = b.dram("XM%d" % l, [NTL * 128, D], F32)
        xf = b.dram("XF%d" % l, [NTL * 128, D], F32)
        cm = b.dram("CM%d" % l, [NTC * 128, D], F32)
        cf = b.dram("CF%d" % l, [NTC * 128, D], F32) if upd else None
        aT_d = b.dram("aT_d%d" % l, [cfg.DFF, NTL * 128], BF16)
        mixer(b, l, x, xk, c, ck, xm, "XM%d_" % l, cm, "CM%d_" % l)
        b.stop("M%d" % l)
        hx = halo_exchange(b, l, xm)
        b.stop("H%d" % l)
        ffn(b, l, xm, "XM%d_" % l, xf, "XF%d_" % l, 0, NTL * 128, NTL * 2, 64, hx, aT_d)
        b.stop("F%d" % l)
        if upd:
            ffn(b, l, cm, "CM%d_" % l, cf, "CF%d_" % l, 1, NTC * 128, 1, NTC * 128, None, aT_d)
            b.stop("FC%d" % l)
            c, ck = cf, "CF%d_" % l
        x, xk = xf, "XF%d_" % l
    final_norm(b, x, xk, NTL)


def prep_inputs(cfg, x, c, ctx, c_ctx, w_mod, b_mod, g_norm1, g_norm2, w_in, gla_w_lr, gla_b_lr, mlstm_b_gate,
                gla_g_norm, mlstm_g_norm, w_out, w_up, conv_w, conv_b, w_down, g_final):
    NC, NB, L, W, FT = cfg.NC, cfg.NB, cfg.L, cfg.W, cfg.FT
    HT = cfg.NTL * 128
    f = lambda a: np.ascontiguousarray(np.asarray(a, dtype=np.float32))
    x, c, ctx, c_ctx = f(x), f(c), f(ctx), f(c_ctx)
    consts = make_consts()
    call = np.concatenate([c, c_ctx[None, :]], axis=0)
    cT = f(call.reshape(NB + 1, KT, 128).transpose(2, 1, 0))
    gcol = f(np.stack([f(g_norm1).reshape(L, KT, 128), f(g_norm2).reshape(L, KT, 128)], axis=1).transpose(0, 3, 1, 2))
    gn = np.concatenate([np.tile(f(gla_g_norm), (1, 4)), np.tile(f(mlstm_g_norm), (1, 4))], axis=1)
    gn_col = f(gn.reshape(L, KT, 128).transpose(0, 2, 1))
    w_in = f(w_in)
    w_main = f(w_in[:, :, :PW - 16])
    wg = w_in[:, :, PW - 16:].reshape(L, D, 2, 8)
    bgate = f(mlstm_b_gate).reshape(L, 2, 8)
    wlr = np.concatenate([f(gla_w_lr), f(gla_b_lr)[:, :, None, :]], axis=2)
    cwf = f(conv_w)
    convb = f(f(conv_b).reshape(L, FT, 128).transpose(0, 2, 1))
    w_out, w_up, w_down, w_mod, b_mod = f(w_out), f(w_up), f(w_down), f(w_mod), f(b_mod)
    gfin = f(g_final).reshape(1, D)
    shared = {}
    in_maps = []
    for core in range(NC):
        bi, half = core // 2, core % 2
        rev = half == 1
        xs = x[bi, half * HT:(half + 1) * HT]
        cx = ctx[bi]
        cw = cwf
        wgc, bgc, wlc = wg, bgate, wlr
        if rev:
            xs, cx = xs[::-1], cx[::-1]
            cw = cwf[:, ::-1, ::-1, :]
            wgc, bgc, wlc = wg[:, :, ::-1, :], bgate[:, ::-1, :], wlr[:, ::-1]
        sel = np.zeros((NB + 1, 2), np.float32)
        sel[bi, 0] = 1.0
        sel[NB, 1] = 1.0
        mk = np.zeros((128, NC), np.float32)
        mk[:, core ^ 1] = 1.0
        key = ("rev", rev)
        if key not in shared:
            shared[key] = dict(
                w_gate=f(wgc.reshape(L, D, 16)), b_gate=f(bgc.reshape(L, 4, 4).transpose(0, 2, 1)), wlr17=f(wlc),
                convw=f(cw.reshape(L, 9, FT, 128).transpose(0, 3, 2, 1)))
        m = dict(x_in=f(xs), ctx_in=f(cx), consts=consts, cT=cT, sel=sel,
                 selbc=f(np.repeat(sel[:, :, None], 128, axis=2)), maskcol=mk,
                 w_mod_sh=f(w_mod[:, :, core * W:(core + 1) * W]), b_mod_sh=f(b_mod[:, core * W:(core + 1) * W]),
                 gcol=gcol, gn_col=gn_col, convb=convb, gfin=gfin, **shared[key])
        if cfg.gather:
            r0, r1 = core * D // NC, (core + 1) * D // NC
            f0, f1 = core * cfg.DFF // NC, (core + 1) * cfg.DFF // NC
            m.update(w_in=f(w_main[:, r0:r1]), w_out=f(w_out[:, r0:r1]), w_up=f(w_up[:, r0:r1]), w_down=f(w_down[:, f0:f1]))
        else:
            m.update(w_in=w_main, w_out=w_out, w_up=w_up, w_down=w_down)
        in_maps.append(m)
    return in_maps


def assemble(cfg, results):
    HT = cfg.NTL * 128
    out = np.zeros((cfg.NB, 2 * HT, D), np.float32)
    for core in range(cfg.NC):
        bi, half = core // 2, core % 2
        y = np.asarray(results[core]["y_out"], dtype=np.float32)
        if half == 1:
            y = y[::-1]
        out[bi, half * HT:(half + 1) * HT] = y
    return out


_NC_CACHE = {}


def run(cfg, inputs):
    key = (cfg.NC, cfg.NTC, cfg.NTL, cfg.DFF, cfg.L, cfg.gather)
    if key not in _NC_CACHE:
        _NC_CACHE[key] = build(cfg)
    nc = _NC_CACHE[key]
    in_maps = prep_inputs(cfg, **inputs)
    res = run_bass_kernel_spmd(nc, in_maps, core_ids=list(range(cfg.NC)))
    return assemble(cfg, res.results)


def kernel(**inputs):
    cfg = Cfg(NC=8, NTC=2, NTL=16, DFF=5504, L=2, gather=True)
    return run(cfg, inputs)
```
